# Optimizing a Trainium2 kernel written in Bass

```python
import jax, jax.numpy as jnp
from jax import lax
import numpy as np

D_MODEL = 1024
BATCH = 8
SEQ = 4096
DEPTH = 2

CHUNK = 64
MIX_WIDTH = D_MODEL
ATT_HEADS = 8
ATT_HEAD_DIM = 64
ATT_WIDTH = ATT_HEADS * ATT_HEAD_DIM
LEFT_CHUNKS = 8
BAND = (LEFT_CHUNKS + 1) * CHUNK
REL_CLIP = 256
RET_HEADS = 4
RET_HEAD_DIM = 128
RET_WIDTH = RET_HEADS * RET_HEAD_DIM
SPLIT_WIDTHS = (ATT_WIDTH, ATT_WIDTH, ATT_WIDTH, ATT_WIDTH,
                RET_WIDTH, RET_WIDTH, RET_WIDTH, RET_WIDTH)
IN_WIDTH = sum(SPLIT_WIDTHS)
EPS = 1e-6
ROPE_BASE = 10000.0
NEG_INF = -1e30

kernel_name = "hybrid_chunked_attn_retention"


def rms_norm(x, gain=None):
    xf = x.astype(jnp.float32)
    y = xf * lax.rsqrt(jnp.mean(xf * xf, axis=-1, keepdims=True) + EPS)
    if gain is not None:
        y = y * gain.astype(jnp.float32)
    return y.astype(x.dtype)


def rotary_tables(positions, dim):
    inv_freq = 1.0 / (ROPE_BASE ** jnp.linspace(0.0, 1.0, dim // 2, dtype=jnp.float32))
    ang = positions.astype(jnp.float32)[:, None] * inv_freq[None, :]
    return jnp.cos(ang), jnp.sin(ang)


def apply_rotary(t, cos, sin):
    t1, t2 = jnp.split(t, 2, axis=-1)
    c = cos[None, :, None, :]
    s = sin[None, :, None, :]
    return jnp.concatenate([t1 * c - t2 * s, t1 * s + t2 * c], axis=-1)


def chunked_rel_attention(q, k, v, rel_bias):
    b, s, h, dh = q.shape
    nc = s // CHUNK
    qc = q.astype(jnp.float32).reshape(b, nc, CHUNK, h, dh)
    pad = ((0, 0), (LEFT_CHUNKS * CHUNK, 0), (0, 0), (0, 0))
    kp = jnp.pad(k.astype(jnp.float32), pad).reshape(b, nc + LEFT_CHUNKS, CHUNK, h, dh)
    vp = jnp.pad(v.astype(jnp.float32), pad).reshape(b, nc + LEFT_CHUNKS, CHUNK, h, dh)
    kb = jnp.concatenate([kp[:, j:j + nc] for j in range(LEFT_CHUNKS + 1)], axis=2)
    vb = jnp.concatenate([vp[:, j:j + nc] for j in range(LEFT_CHUNKS + 1)], axis=2)
    scores = jnp.einsum("bnqhd,bnkhd->bnhqk", qc, kb) * (dh ** -0.5)
    q_off = jnp.arange(CHUNK)[:, None] + LEFT_CHUNKS * CHUNK
    k_off = jnp.arange(BAND)[None, :]
    dist = q_off - k_off
    bias = rel_bias.astype(jnp.float32)[:, jnp.clip(dist, -REL_CLIP, REL_CLIP) + REL_CLIP]
    key_pos = (jnp.arange(nc)[:, None] - LEFT_CHUNKS) * CHUNK + k_off
    valid = (key_pos >= 0)[None, :, None, None, :]
    scores = jnp.where(valid, scores + bias[None, None], NEG_INF)
    p = jax.nn.softmax(scores, axis=-1)
    out = jnp.einsum("bnhqk,bnkhd->bnqhd", p, vb)
    return out.reshape(b, s, h * dh)


def chunkwise_retention(q, k, v, cos, sin):
    b, s, h, dk = q.shape
    dv = v.shape[-1]
    nc = s // CHUNK
    q = apply_rotary(q.astype(jnp.float32), cos, sin)
    k = apply_rotary(k.astype(jnp.float32), cos, sin) * (dk ** -0.5)
    v = v.astype(jnp.float32)
    log_g = jnp.log1p(-jnp.exp2(-5.0 - jnp.arange(h, dtype=jnp.float32)))
    idx = jnp.arange(CHUNK, dtype=jnp.float32)
    intra_decay = jnp.exp(jnp.abs(idx[:, None] - idx[None, :])[None] * log_g[:, None, None])
    k_decay = jnp.exp((CHUNK - 1 - idx)[None, :] * log_g[:, None])
    q_decay = jnp.exp((idx + 1.0)[None, :] * log_g[:, None])
    chunk_decay = jnp.exp(CHUNK * log_g)
    qc = q.reshape(b, nc, CHUNK, h, dk)
    kc = k.reshape(b, nc, CHUNK, h, dk)
    vc = v.reshape(b, nc, CHUNK, h, dv)
    att = jnp.einsum("bnihd,bnjhd->bnhij", qc, kc) * intra_decay[None, None]
    o_intra = jnp.einsum("bnhij,bnjhe->bnihe", att, vc)
    kv = jnp.einsum("bnjhd,hj,bnjhe->bnhde", kc, k_decay, vc)

    def step(state, kv_n):
        return state * chunk_decay[None, :, None, None] + kv_n, state

    _, states = lax.scan(step, jnp.zeros((b, h, dk, dv), jnp.float32), jnp.moveaxis(kv, 1, 0))
    states = jnp.moveaxis(states, 0, 1)
    o_inter = jnp.einsum("bnihd,hi,bnhde->bnihe", qc, q_decay, states)
    o = rms_norm(o_intra + o_inter)
    return o.reshape(b, s, h * dv)


def hybrid_layer(x, norm_gain, w_in, w_out, rel_bias, cos, sin):
    b, s, _ = x.shape
    hn = rms_norm(x, norm_gain)
    proj = hn @ w_in.astype(hn.dtype)
    cuts = list(np.cumsum(SPLIT_WIDTHS)[:-1])
    aq, ak, av, ag, rq, rk, rv, rg = jnp.split(proj, cuts, axis=-1)
    ya = chunked_rel_attention(aq.reshape(b, s, ATT_HEADS, ATT_HEAD_DIM),
                               ak.reshape(b, s, ATT_HEADS, ATT_HEAD_DIM),
                               av.reshape(b, s, ATT_HEADS, ATT_HEAD_DIM), rel_bias)
    yr = chunkwise_retention(rq.reshape(b, s, RET_HEADS, RET_HEAD_DIM),
                             rk.reshape(b, s, RET_HEADS, RET_HEAD_DIM),
                             rv.reshape(b, s, RET_HEADS, RET_HEAD_DIM), cos, sin)
    y = jnp.concatenate([ya * jax.nn.silu(ag.astype(jnp.float32)),
                         yr * jax.nn.silu(rg.astype(jnp.float32))], axis=-1).astype(x.dtype)
    return x + y @ w_out.astype(x.dtype)


def setup_inputs(seed: int = 0) -> dict:
    key = jax.random.key(seed)
    kx, kg, ki, ko, kb, kf = jax.random.split(key, 6)
    x = jax.random.normal(kx, (BATCH, SEQ, D_MODEL), jnp.float32)
    positions = jnp.arange(SEQ, dtype=jnp.int32)
    norm_gain = 1.0 + 0.02 * jax.random.normal(kg, (DEPTH, D_MODEL), jnp.float32)
    w_in = jax.random.normal(ki, (DEPTH, D_MODEL, IN_WIDTH), jnp.float32) * D_MODEL ** -0.5
    w_out = jax.random.normal(ko, (DEPTH, MIX_WIDTH, D_MODEL), jnp.float32) * MIX_WIDTH ** -0.5
    rel_bias = 0.1 * jax.random.normal(kb, (DEPTH, ATT_HEADS, 2 * REL_CLIP + 1), jnp.float32)
    final_gain = 1.0 + 0.02 * jax.random.normal(kf, (D_MODEL,), jnp.float32)
    return {"x": x, "positions": positions, "norm_gain": norm_gain, "w_in": w_in,
            "w_out": w_out, "rel_bias": rel_bias, "final_gain": final_gain}


def reference(x, positions, norm_gain, w_in, w_out, rel_bias, final_gain):
    cos, sin = rotary_tables(positions, RET_HEAD_DIM)
    for layer in range(DEPTH):
        x = hybrid_layer(x, norm_gain[layer], w_in[layer], w_out[layer], rel_bias[layer], cos, sin)
    return rms_norm(x, final_gain)
```

```python
import numpy as np
import concourse.bass as bass
import concourse.mybir as mybir
from concourse.bass_utils import run_bass_kernel_spmd

F32 = mybir.dt.float32
BF16 = mybir.dt.bfloat16
I32 = mybir.dt.int32
AF = mybir.ActivationFunctionType
ALU = mybir.AluOpType
AX = mybir.AxisListType

D = 1024
SEQ = 4096
NTILES = 32
NCORES = 8
EPS = 1e-6
TWO_PI = 6.283185307179586
C1 = 6.28125
C2 = TWO_PI - C1
NEG = -30000.0
NSLOT = 6


class _Op:
    __slots__ = ("eng", "fn", "deps", "chan", "signal", "key", "cnt")


class Sched:
    def __init__(self, nc):
        self.nc = nc
        self.engs = {"pe": nc.tensor, "act": nc.scalar, "dve": nc.vector,
                     "pool": nc.gpsimd, "sp": nc.sync}
        self.ops = []
        self.lw = {}
        self.rd = {}
        self.chans = set()

    def add(self, eng, fn, reads=(), writes=(), chan=None):
        op = _Op()
        op.eng, op.fn, op.chan, op.signal = eng, fn, chan, False
        op.key, op.cnt = None, 0
        deps = set()
        for r in reads:
            w = self.lw.get(r)
            if w is not None:
                deps.add(w)
        for wn in writes:
            w = self.lw.get(wn)
            if w is not None:
                deps.add(w)
            for r in self.rd.get(wn, ()):
                deps.add(r)
        op.deps = deps
        for r in reads:
            self.rd.setdefault(r, []).append(op)
        for wn in writes:
            self.lw[wn] = op
            self.rd[wn] = []
        if chan is not None:
            self.chans.add(chan)
        self.ops.append(op)
        return op

    @staticmethod
    def _skip(d, op):
        return d.eng == "pe" and op.eng == "pe" and d.chan is None and op.chan is None

    def emit(self, sems, final_engine="sp"):
        index = {id(op): i for i, op in enumerate(self.ops)}
        for op in self.ops:
            newest = {}
            keep = set()
            for d in op.deps:
                if self._skip(d, op):
                    continue
                if d.chan is None and d.eng in ("pe", "act", "dve"):
                    cur = newest.get(d.eng)
                    if cur is None or index[id(d)] > index[id(cur)]:
                        newest[d.eng] = d
                else:
                    keep.add(d)
            keep.update(newest.values())
            for d in keep:
                d.signal = True
            op.deps = keep
        counters = {}
        waited = {e: {} for e in self.engs}
        last_out = {}
        for op in self.ops:
            engine = self.engs[op.eng]
            need = {}
            for d in op.deps:
                if self._skip(d, op):
                    continue
                if need.get(d.key, 0) < d.cnt:
                    need[d.key] = d.cnt
            for key, val in need.items():
                if waited[op.eng].get(key, 0) >= val:
                    continue
                engine.wait_ge(sems[key], val)
                waited[op.eng][key] = val
            inst = op.fn()
            if op.signal or op.chan is not None:
                if op.chan is not None:
                    key, inc = op.chan, 16
                else:
                    key, inc = op.eng, 1
                counters[key] = counters.get(key, 0) + inc
                inst.then_inc(sems[key], inc)
                op.key, op.cnt = key, counters[key]
                if op.chan is not None and op.chan.startswith("out"):
                    last_out[key] = counters[key]
        fe = self.engs[final_engine]
        for key, val in last_out.items():
            fe.wait_ge(sems[key], val)


def _rev_last(ap):
    pat = [list(x) for x in ap.ap]
    step, n = pat[-1]
    pat[-1] = [-step, n]
    return bass.AP(ap.tensor, ap.offset + (n - 1) * step, pat)


def _consts():
    c = {}
    c["ident"] = np.eye(128, dtype=np.float32)
    c["invf"] = (np.float32(1.0) / np.power(np.float32(10000.0),
                 np.linspace(0.0, 1.0, 64, dtype=np.float32))).astype(np.float32)
    h = np.arange(4, dtype=np.float64)
    lg = np.log1p(-np.exp2(-5.0 - h))
    s = 128.0 ** -0.5
    i = np.arange(128, dtype=np.float64)
    j = np.arange(128, dtype=np.float64)
    dp = np.zeros((128, 4, 128), np.float64)
    for hh in range(4):
        same = (j[:, None] // 64) == (i[None, :] // 64)
        cross = (j[:, None] < 64) & (i[None, :] >= 64)
        dist = np.where(same, np.abs(i[None, :] - j[:, None]), i[None, :] - j[:, None])
        val = np.exp(lg[hh] * (dist - (i[None, :] + 1.0))) * s
        dp[:, hh, :] = np.where(same | cross, val, 0.0)
    c["dp"] = dp.astype(np.float32)
    c["kdec"] = (np.exp(lg[None, :] * (127.0 - j[:, None])) * s).astype(np.float32)
    g2 = np.exp(2.0 * lg[None, :] * (i[:, None] + 1.0))
    c["epsg"] = (4.0 * EPS / g2).astype(np.float32)
    c["sdec"] = [float(np.exp(128.0 * lg[hh])) for hh in range(4)]
    m = np.zeros((128, 5, 128), np.float32)
    m[64:128, 4, 0:64] = NEG
    c["mask"] = m
    return c


def build_program(layers=(0, 1), nt=NTILES, final=True, x_in_name="x", dbg=False):
    nc = bass.Bass("TRN2", target_bir_lowering=False, dynamic_dma_scratch_size=16384)
    cst = _consts()
    ntok = nt * 128
    x_d = nc.dram_tensor("x", [ntok, D], F32, kind="ExternalInput").ap()
    pos_d = nc.dram_tensor("pos", [ntok], I32, kind="ExternalInput").ap()
    ng_d = nc.dram_tensor("norm_gain", [2, D], F32, kind="ExternalInput").ap()
    win_d = nc.dram_tensor("w_in", [2, D, 4096], F32, kind="ExternalInput").ap()
    wout_d = nc.dram_tensor("w_out", [2, D, D], F32, kind="ExternalInput").ap()
    rbr_d = nc.dram_tensor("rbr", [16, 513], F32, kind="ExternalInput").ap()
    fg_d = nc.dram_tensor("final_gain", [D], F32, kind="ExternalInput").ap()
    ident_d = nc.dram_tensor("c_ident", [128, 128], F32, kind="ExternalInput").ap()
    invf_d = nc.dram_tensor("c_invf", [64], F32, kind="ExternalInput").ap()
    dp_d = nc.dram_tensor("c_dp", [128, 4, 128], F32, kind="ExternalInput").ap()
    kdec_d = nc.dram_tensor("c_kdec", [128, 4], F32, kind="ExternalInput").ap()
    epsg_d = nc.dram_tensor("c_epsg", [128, 4], F32, kind="ExternalInput").ap()
    mask_d = nc.dram_tensor("c_mask", [128, 5, 128], F32, kind="ExternalInput").ap()
    y_d = nc.dram_tensor("y", [ntok, D], F32, kind="ExternalOutput").ap()
    x1_t = nc.dram_tensor("x1s", [ntok, D], F32, kind="Internal")
    x1_d = x1_t.ap()
    gr_t = nc.dram_tensor("grs", [16, 768], F32, kind="Internal")
    gr_d = gr_t.ap()

    from contextlib import ExitStack
    es = ExitStack()

    def sb(name, shape, dt):
        return es.enter_context(nc.sbuf_tensor(name, shape, dt))

    with es:
        win = sb("win", [128, 8, 4096], BF16)
        wout = sb("wout", [128, 8, 1024], BF16)
        xt = [sb(f"xt{i}", [128, 1024], F32) for i in range(3)]
        hn = sb("hn", [128, 1024], BF16)
        junk = sb("junk", [128, 1024], BF16)
        hnT = [sb(f"hnT{i}", [128, 8, 128], BF16) for i in range(2)]
        gB = [sb(f"gB{l}", [128, 1024], F32) for l in range(2)]
        fgB = sb("fgB", [128, 1024], F32)
        qTa = [sb(f"qTa{i}", [128, 4, 128], BF16) for i in range(2)]
        kTr = sb("kTr", [128, 4, NSLOT, 128], BF16)
        Vr = sb("Vr", [128, NSLOT, 8, 65], BF16)
        sga = [sb(f"sga{i}", [128, 512], F32) for i in range(2)]
        sgr = [sb(f"sgr{i}", [128, 512], F32) for i in range(2)]
        btab = [sb(f"btab{l}", [128, 8, 5, 128], BF16) for l in range(2)]
        PT = [sb(f"PT{i}", [128, 5, 128], BF16) for i in range(4)]
        cs = sb("cs", [128, 2, 32, 64], F32)
        cosT = cs[:, 0, 0:nt, :]
        sinT = cs[:, 1, 0:nt, :]
        mb = sb("mb", [128, 8, 2, 64], F32)
        qkrot = [sb(f"qkrot{i}", [128, 8, 128], BF16) for i in range(2)]
        qkT = sb("qkT", [128, 8, 128], BF16)
        Kd = sb("Kd", [128, 4, 128], BF16)
        Vret = [sb(f"Vret{i}", [128, 4, 128], BF16) for i in range(2)]
        Dp = sb("Dp", [128, 4, 128], F32)
        attm = sb("attm", [128, 4, 128], BF16)
        S = sb("S", [128, 4, 128], F32)
        Sbf = sb("Sbf", [128, 4, 128], BF16)
        sq = sb("sq", [128, 4, 128], F32)
        ybuf = sb("ybuf", [128, 1024], BF16)
        yT = sb("yT", [128, 8, 128], BF16)
        tmpa = sb("tmpa", [128, 4 * 128], F32)
        tmpr = sb("tmpr", [128, 4, 128], F32)
        identf = sb("identf", [128, 128], F32)
        ident = sb("ident", [128, 128], BF16)
        small = sb("small", [128, 64], F32)
        kdec = sb("kdec", [128, 4], F32)
        epsg = sb("epsg", [128, 4], F32)
        cm05 = sb("cm05", [128, 4], F32)
        mcol = sb("mcol", [128, 1], F32)
        posi = sb("posi", [nt, 128], I32)
        posr = sb("posr", [nt, 128], F32)
        posf = sb("posf", [128, nt], F32)
        invfB = sb("invfB", [128, 64], F32)
        wv = wout[:].rearrange("p c f -> p (c f)").bitcast(F32)
        nF = wv[:, 0:nt * 64].rearrange("p (t f) -> p t f", f=64)
        nI = wv[:, 2048:2048 + nt * 64].bitcast(I32).rearrange("p (t f) -> p t f", f=64)
        stage = [xt[1][:, 0:640].rearrange("p (c q) -> p c q", q=128),
                 xt[2][:, 0:640].rearrange("p (c q) -> p c q", q=128)]
        stgn = ["xt1", "xt2"]
        rb = xt[1][0:16, 640:1024]
        zer = xt[2][0:16, 640:1023]
        maskT = mb[:].rearrange("p h two f -> p (h two f)")[:, 0:640].rearrange("p (c q) -> p c q", q=128)
        ch0 = small[0:16, 32:33]
        ssq = small[:, 0:1]
        u0 = small[:, 1:2]
        rstd = small[:, 2:3]
        ssq2 = small[:, 3:4]
        u2 = small[:, 4:5]
        rstd2 = small[:, 5:6]
        rden = small[:, 8:16]
        ssr = small[:, 16:20]
        ur = small[:, 20:24]
        rr = small[:, 24:28]

        pf = es.enter_context(nc.psum_tensor("pf", [128, 8 * 512], F32))

        def tb(b):
            return pf[:, b * 512:(b + 1) * 512].bitcast(BF16)

        sch = Sched(nc)
        A = sch.add
        V, ACT, PE, POOL, SP = nc.vector, nc.scalar, nc.tensor, nc.gpsimd, nc.sync

        GEN = [4, 5, 6, 7]
        busy = set()
        freed_at = {b: -1 for b in GEN}
        clock = [0]

        def alloc1():
            cand = [b for b in GEN if b not in busy]
            if not cand:
                raise RuntimeError("no free PSUM bank (schedule keeps too many live)")
            b = min(cand, key=lambda x: freed_at[x])
            busy.add(b)
            return b

        def alloc2():
            cand = [b for b in GEN[:-1] if b not in busy and (b + 1) not in busy]
            if not cand:
                raise RuntimeError("no free PSUM bank pair")
            b = min(cand, key=lambda x: max(freed_at[x], freed_at[x + 1]))
            busy.add(b)
            busy.add(b + 1)
            return b

        def free(b, n=1):
            clock[0] += 1
            for i in range(n):
                busy.discard(b + i)
                freed_at[b + i] = clock[0]

        def bk(b, n=1):
            return pf[:, b * 512:(b + n) * 512]

        def bnames(b, n=1):
            return [f"pf{b + i}" for i in range(n)]

        dma_id = [0]

        def dma(eng, out, in_, reads, writes, chan=None, **kw):
            if chan is None:
                chan = f"d{dma_id[0]}"
                dma_id[0] += 1
            e = {"sp": SP, "pool": POOL, "act": ACT}[eng]
            return A(eng, lambda: e.dma_start(out=out, in_=in_, **kw), reads, writes, chan=chan)

        dma("sp", identf[:], ident_d, [], ["identf"])
        dma("sp", xt[0][:], x_d[0:128, :], [], ["xt0"], chan="xl0")
        dma("sp", rb, rbr_d[:, 0:384], ["xt1"], ["rb"])
        dma("sp", maskT, mask_d, ["mb"], ["maskT"])
        dma("sp", posi[:], pos_d.rearrange("(t p) -> t p", p=128), [], ["posi"])
        dma("sp", invfB[:], invf_d.partition_broadcast(128), [], ["invfB"])
        dma("sp", Dp[:], dp_d, [], ["Dp"])
        dma("sp", kdec[:], kdec_d, [], ["kdec"])
        dma("sp", epsg[:], epsg_d, [], ["epsg"])
        for l in range(2):
            dma("sp", gB[l][:], ng_d[l].partition_broadcast(128), [], [f"gB{l}"])
        dma("sp", fgB[:], fg_d.partition_broadcast(128), [], ["fgB"])

        A("dve", lambda: V.tensor_copy(out=ident[:], in_=identf[:]), ["identf"], ["ident"])
        A("pool", lambda: POOL.memset(cm05[:], -0.5), [], ["cm05"])
        A("pool", lambda: [POOL.memset(mcol[64:128, :], 0.0), POOL.memset(mcol[0:64, :], NEG)][-1], [], ["mcol"])
        A("pool", lambda: POOL.memset(Vr[:], 2.0), [], [f"V{s}" for s in range(NSLOT)])

        A("dve", lambda: V.memset(zer, 0.0), ["xt2"], ["zer"])
        A("dve", lambda: V.tensor_copy(out=ch0, in_=rb[:, 0:1]), ["rb"], ["ch0"])
        A("dve", lambda: V.tensor_scalar(out=rb, in0=rb, scalar1=ch0, scalar2=None, op0=ALU.subtract),
          ["rb", "ch0", "xt1"], ["rb"])
        dma("sp", gr_d[:, 0:383], zer, ["zer", "xt2"], ["gr0"])
        dma("sp", gr_d[:, 383:767], rb, ["rb", "xt1"], ["gr1"])

        def bias_dma(lh):
            src = bass.AP(gr_t, lh * 768, [[1, 128], [128, 5], [1, 128]])
            dma("sp", stage[lh % 2], src, ["gr0", "gr1", stgn[lh % 2]], [f"stage{lh % 2}"], chan=f"stg{lh % 2}")

        def bias_add(lh):
            l_, h_ = lh // 8, lh % 8
            st = stage[lh % 2]
            A("dve", lambda: V.tensor_tensor(out=btab[l_][:, h_, :, :], in0=_rev_last(st), in1=maskT, op=ALU.add),
              [f"stage{lh % 2}", "maskT", "mb", stgn[lh % 2]], [f"btab{l_}"])

        bias_state = [0]
        bias_dma(0)
        bias_dma(1)

        def bias_step():
            k = bias_state[0]
            if k < 16:
                bias_add(k)
                if k + 2 < 16:
                    bias_dma(k + 2)
                bias_state[0] += 1

        def AD(fn, reads, writes):
            A("dve", fn, reads, writes)
            bias_step()

        A("dve", lambda: V.tensor_copy(out=posr[:], in_=posi[:]), ["posi"], ["posr"])
        A("pe", lambda: PE.transpose(pf[:, 0:nt], posr[:], identf[0:nt, 0:nt]), ["posr", "identf"], ["pf0"])
        A("dve", lambda: V.tensor_copy(out=posf[:], in_=pf[:, 0:nt]), ["pf0"], ["posf"])

        AD(lambda: V.tensor_tensor(out=cosT, in0=posf[:].unsqueeze(2).broadcast_to([128, nt, 64]),
                                   in1=invfB[:].unsqueeze(1).broadcast_to([128, nt, 64]),
                                   op=ALU.mult), ["posf", "invfB"], ["cosT"])

        def sin_table(dst, src, shift, name, srcname):
            if shift != 0.0:
                AD(lambda: V.tensor_scalar(out=src, in0=src, scalar1=float(shift), scalar2=None,
                                           op0=ALU.add), [srcname], [srcname])
            AD(lambda: V.tensor_scalar(out=nF, in0=src, scalar1=1.0 / TWO_PI, scalar2=None,
                                       op0=ALU.mult), [srcname], ["nF", "wout"])
            AD(lambda: V.tensor_copy(out=nI, in_=nF), ["nF"], ["nI", "wout"])
            AD(lambda: V.tensor_copy(out=nF, in_=nI), ["nI"], ["nF", "wout"])
            AD(lambda: V.scalar_tensor_tensor(out=dst, in0=nF, scalar=-C1, in1=src,
                                              op0=ALU.mult, op1=ALU.add), ["nF", srcname, "wout"], [name])
            AD(lambda: V.scalar_tensor_tensor(out=dst, in0=nF, scalar=-C2, in1=dst,
                                              op0=ALU.mult, op1=ALU.add), ["nF", name, "wout"], [name])
            AD(lambda: V.tensor_scalar(out=dst, in0=dst, scalar1=3.1415925, scalar2=-3.1415925,
                                       op0=ALU.min, op1=ALU.max), [name], [name])
            A("act", lambda: ACT.activation(out=dst, in_=dst, func=AF.Sin), [name], [name])

        sin_table(sinT, cosT, 0.0, "sinT", "cosT")
        sin_table(cosT, cosT, np.pi / 2, "cosT", "cosT")
        while bias_state[0] < 16:
            bias_step()

        def load_wout(l):
            dma("pool", wout[:], wout_d[l].rearrange("(c p) f -> p c f", p=128), [], ["wout"],
                chan="wo", max_dma_last_dim=8192)

        def load_win(l, groups=range(8)):
            src = win_d[l].rearrange("(c p) f -> p c f", p=128)
            for g in groups:
                dma("pool", win[:, :, g * 512:(g + 1) * 512], src[:, :, g * 512:(g + 1) * 512],
                    [], [f"win{g}"], chan=f"wi{g}", max_dma_last_dim=8192)

        st8 = {}

        def load_x(li, t):
            b = t % 3
            src = (x_d if li == 0 else x1_d)[t * 128:(t + 1) * 128, :]
            rds = [] if li == 0 else [f"x1_{t}"]
            dma("sp", xt[b][:], src, rds, [f"xt{b}"], chan=f"xl{b}")

        def A_norm_a(l, t):
            xb, xn = xt[t % 3], f"xt{t % 3}"
            A("act", lambda: ACT.activation(out=junk[:], in_=xb[:], func=AF.Square, accum_out=ssq),
              [xn], ["junk", "ssq"])
            A("pool", lambda: POOL.tensor_scalar(out=u0, in0=ssq, scalar1=1.0 / D, scalar2=EPS,
                                                 op0=ALU.mult, op1=ALU.add), ["ssq"], ["u0"])
            A("pool", lambda: POOL.tensor_tensor(out=rstd, in0=u0, in1=cm05[:, 0:1], op=ALU.pow),
              ["u0", "cm05"], ["rstd"])

        def A_norm_d(l, t):
            xb, xn = xt[t % 3], f"xt{t % 3}"
            A("dve", lambda: V.scalar_tensor_tensor(out=hn[:], in0=xb[:], scalar=rstd, in1=gB[l][:],
                                                    op0=ALU.mult, op1=ALU.mult),
              [xn, "rstd", f"gB{l}"], ["hn"])

        def A_tr(l, t):
            p2 = t % 2
            bt = alloc1()
            A("pe", lambda: [PE.transpose(tb(bt)[:, c * 128:(c + 1) * 128], hn[:, c * 128:(c + 1) * 128], ident[:])
                             for c in range(8)][-1], ["hn", "ident"], bnames(bt))
            A("act", lambda: ACT.copy(out=hnT[p2][:].rearrange("p c t -> p (c t)"), in_=tb(bt)),
              bnames(bt), [f"hnT{p2}"])
            free(bt)

        def mm_tm(b0, g, hT):
            last = None
            for c in range(8):
                last = PE.matmul(pf[:, b0 * 512:(b0 + 1) * 512], hT[:, c, :],
                                 win[:, c, g * 512:(g + 1) * 512], start=(c == 0), stop=(c == 7))
            return last

        def B(l, t, g):
            p2 = t % 2
            slot = t % NSLOT
            hT, hTn = hnT[p2], f"hnT{p2}"
            if g in (0, 1):
                bq = alloc1()

                def mm_fm():
                    last = None
                    for fc in range(4):
                        f0 = (g * 4 + fc) * 128
                        for c in range(8):
                            last = PE.matmul(pf[:, bq * 512 + fc * 128: bq * 512 + (fc + 1) * 128],
                                             win[:, c, f0:f0 + 128], hT[:, c, :],
                                             start=(c == 0), stop=(c == 7))
                    return last
                A("pe", mm_fm, [hTn, f"win{g}"], bnames(bq))
                if g == 0:
                    A("act", lambda: ACT.mul(out=qTa[p2][:].rearrange("p a t -> p (a t)"),
                                             in_=bk(bq), mul=0.125), bnames(bq), [f"qTa{p2}"])
                else:
                    A("act", lambda: ACT.copy(out=kTr[:, :, slot, :],
                                              in_=bk(bq).rearrange("p (a t) -> p a t", a=4)),
                      bnames(bq), [f"kT{slot}"])
                free(bq)
            elif g == 2:
                bv = alloc1()
                A("pe", lambda: mm_tm(bv, 2, hT), [hTn, "win2"], bnames(bv))
                A("dve", lambda: V.tensor_copy(out=Vr[:, slot, :, 0:64],
                                               in_=bk(bv).rearrange("p (h e) -> p h e", h=8)),
                  bnames(bv), [f"V{slot}"])
                free(bv)
            elif g in (3, 7):
                bg = alloc1()
                sg = (sga if g == 3 else sgr)[p2]
                sgn = ("sga" if g == 3 else "sgr") + str(p2)
                A("pe", lambda: mm_tm(bg, g, hT), [hTn, f"win{g}"], bnames(bg))
                A("act", lambda: ACT.activation(out=sg[:], in_=bk(bg), func=AF.Tanh, scale=0.5),
                  bnames(bg), [sgn])
                A("dve", lambda: V.scalar_tensor_tensor(out=sg[:], in0=sg[:], scalar=1.0, in1=bk(bg),
                                                        op0=ALU.add, op1=ALU.mult),
                  [sgn] + bnames(bg), [sgn])
                free(bg)
            elif g == 4:
                br = alloc2()
                st8[(l, t, "br")] = br
                A("pe", lambda: mm_tm(br, 4, hT), [hTn, "win4"], bnames(br))
            elif g == 5:
                br = st8[(l, t, "br")]
                A("pe", lambda: mm_tm(br + 1, 5, hT), [hTn, "win5"], bnames(br + 1))
            elif g == 55:
                br = st8[(l, t, "br")]
                v4 = bk(br, 2).rearrange("p (h two f) -> p h two f", h=8, two=2)
                pat = [list(x) for x in v4.ap]
                pat[2] = [-pat[2][0], 2]
                v4s = bass.AP(v4.tensor, v4.offset + 64, pat)
                A("dve", lambda: V.tensor_tensor(
                    out=mb[:], in0=v4s,
                    in1=sinT[:, t, :].unsqueeze(1).unsqueeze(1).broadcast_to([128, 8, 2, 64]),
                    op=ALU.mult), bnames(br, 2) + ["sinT"], ["mb"])
                A("dve", lambda: V.tensor_tensor(
                    out=v4, in0=v4,
                    in1=cosT[:, t, :].unsqueeze(1).unsqueeze(1).broadcast_to([128, 8, 2, 64]),
                    op=ALU.mult), bnames(br, 2) + ["cosT"], bnames(br, 2))
            elif g == 56:
                br = st8[(l, t, "br")]
                v4 = bk(br, 2).rearrange("p (h two f) -> p h two f", h=8, two=2)
                qk4 = qkrot[p2][:].rearrange("p h (two f) -> p h two f", two=2)
                A("dve", lambda: V.tensor_tensor(out=qk4[:, :, 0, :], in0=v4[:, :, 0, :], in1=mb[:, :, 0, :],
                                                 op=ALU.subtract), bnames(br, 2) + ["mb"], [f"qkrot{p2}"])
                A("dve", lambda: V.tensor_tensor(out=qk4[:, :, 1, :], in0=v4[:, :, 1, :], in1=mb[:, :, 1, :],
                                                 op=ALU.add), bnames(br, 2) + ["mb"], [f"qkrot{p2}"])
            elif g == 57:
                free(st8[(l, t, "br")], 2)
                A("pool", lambda: POOL.tensor_tensor(out=Kd[:], in0=qkrot[p2][:, 4:8, :],
                                                     in1=kdec[:].unsqueeze(2).broadcast_to([128, 4, 128]),
                                                     op=ALU.mult), [f"qkrot{p2}", "kdec"], ["Kd"])
            elif g == 6:
                bv2 = alloc1()
                A("pe", lambda: mm_tm(bv2, 6, hT), [hTn, "win6"], bnames(bv2))
                A("act", lambda: ACT.copy(out=Vret[p2][:].rearrange("p h e -> p (h e)"), in_=bk(bv2)),
                  bnames(bv2), [f"Vret{p2}"])
                free(bv2)

        def att_info(t):
            nk = min(t, 4) + 1
            c0 = 5 - nk
            chunks = list(range(c0, 5))
            kslot = {c: (t - 4 + c) % NSLOT for c in chunks}
            return c0, chunks, kslot

        def C_scores(l, t, h):
            p2 = t % 2
            c0, chunks, kslot = att_info(t)
            hp, e = h // 2, h % 2
            r0 = 64 * e
            BS = 2 * e
            ptb, ptn = PT[h % 4], f"PT{h % 4}"

            def mm_sc():
                last = None
                for c in chunks:
                    last = PE.matmul(pf[:, BS * 512 + c * 128: BS * 512 + (c + 1) * 128],
                                     kTr[r0:r0 + 64, hp, kslot[c], :], qTa[p2][r0:r0 + 64, hp, :],
                                     start=True, stop=True)
                return last
            A("pe", mm_sc, [f"qTa{p2}"] + [f"kT{kslot[c]}" for c in chunks], bnames(BS, 2))
            cb = max(c0, 2)

            def bias_ops():
                last = V.tensor_tensor(
                    out=pf[:, BS * 512 + cb * 128: BS * 512 + 640],
                    in0=pf[:, BS * 512 + cb * 128: BS * 512 + 640],
                    in1=btab[l][:, h, cb:5, :].rearrange("p c q -> p (c q)"), op=ALU.add)
                if c0 == 0:
                    last = V.tensor_scalar(out=pf[:, BS * 512 + 64: BS * 512 + 128],
                                           in0=pf[:, BS * 512 + 64: BS * 512 + 128],
                                           scalar1=mcol[:, 0:1], scalar2=None, op0=ALU.add)
                return last
            A("dve", bias_ops, bnames(BS, 2) + [f"btab{l}", "mcol"], bnames(BS, 2))
            A("act", lambda: ACT.activation(
                out=ptb[:, c0:5, :].rearrange("p c q -> p (c q)"),
                in_=pf[:, BS * 512 + c0 * 128: BS * 512 + 640], func=AF.Exp),
              bnames(BS, 2), [ptn])

        def C_scores2(l, t, h0):
            p2 = t % 2
            c0, chunks, kslot = att_info(t)
            hp = h0 // 2

            for e in range(2):
                def mm_sc(e=e):
                    last = None
                    r0 = 64 * e
                    BS = 2 * e
                    for c in chunks:
                        last = PE.matmul(pf[:, BS * 512 + c * 128: BS * 512 + (c + 1) * 128],
                                         kTr[r0:r0 + 64, hp, kslot[c], :], qTa[p2][r0:r0 + 64, hp, :],
                                         start=True, stop=True)
                    return last
                A("pe", mm_sc, [f"qTa{p2}"] + [f"kT{kslot[c]}" for c in chunks], bnames(2 * e, 2))
            cb = max(c0, 2)
            for e in range(2):
                h = h0 + e
                BS = 2 * e
                ptb, ptn = PT[h % 4], f"PT{h % 4}"

                def bias_ops(BS=BS, h=h):
                    last = V.tensor_tensor(
                        out=pf[:, BS * 512 + cb * 128: BS * 512 + 640],
                        in0=pf[:, BS * 512 + cb * 128: BS * 512 + 640],
                        in1=btab[l][:, h, cb:5, :].rearrange("p c q -> p (c q)"), op=ALU.add)
                    if c0 == 0:
                        last = V.tensor_scalar(out=pf[:, BS * 512 + 64: BS * 512 + 128],
                                               in0=pf[:, BS * 512 + 64: BS * 512 + 128],
                                               scalar1=mcol[:, 0:1], scalar2=None, op0=ALU.add)
                    return last
                A("dve", bias_ops, bnames(BS, 2) + [f"btab{l}", "mcol"], bnames(BS, 2))
            for e in range(2):
                h = h0 + e
                BS = 2 * e
                ptb, ptn = PT[h % 4], f"PT{h % 4}"
                A("act", lambda BS=BS, ptb=ptb: ACT.activation(
                    out=ptb[:, c0:5, :].rearrange("p c q -> p (c q)"),
                    in_=pf[:, BS * 512 + c0 * 128: BS * 512 + 640], func=AF.Exp),
                  bnames(BS, 2), [ptn])

        def C_pv(l, t, h):
            c0, chunks, kslot = att_info(t)
            ptb, ptn = PT[h % 4], f"PT{h % 4}"
            BS = 2 * (h % 2)

            def mm_pv():
                last = None
                o0 = BS * 512 + 640 + (h // 2) * 65
                for c in chunks:
                    last = PE.matmul(pf[:, o0:o0 + 65], ptb[:, c, :], Vr[:, kslot[c], h, :],
                                     start=(c == c0), stop=(c == 4))
                return last
            A("pe", mm_pv, [ptn] + [f"V{kslot[c]}" for c in chunks], [f"pf{BS + 1}"])

        def C_fin(l, t, e):
            tmpa4 = tmpa[:].rearrange("p (j e d) -> p j e d", e=2, d=64)
            o0 = 2 * e * 512 + 640
            pv = pf[:, o0:o0 + 260].rearrange("p (h e) -> p h e", e=65)
            A("dve", lambda: V.reciprocal(out=rden[:, e * 4:(e + 1) * 4].unsqueeze(2), in_=pv[:, :, 64:65]),
              ["pf1", "pf3"], [f"rden{e}"])
            A("dve", lambda: V.tensor_tensor(
                out=tmpa4[:, :, e, :], in0=pv[:, :, 0:64],
                in1=rden[:, e * 4:(e + 1) * 4].unsqueeze(2).broadcast_to([128, 4, 64]),
                op=ALU.mult), [f"pf{2 * e + 1}", f"rden{e}"], [f"tmpa{e}"])

        def C_gate(l, t):
            p2 = t % 2
            A("pool", lambda: POOL.tensor_tensor(out=ybuf[:, 0:512], in0=tmpa[:], in1=sga[p2][:], op=ALU.mult),
              ["tmpa0", "tmpa1", f"sga{p2}"], ["ya"])

        def D1(l, t):
            p2 = t % 2
            bt = alloc1()
            A("pe", lambda: [PE.transpose(tb(bt)[:, c * 128:(c + 1) * 128], qkrot[p2][:, c, :], ident[:])
                             for c in range(8)][-1], [f"qkrot{p2}", "ident"], bnames(bt))
            A("act", lambda: ACT.copy(out=qkT[:].rearrange("p c t -> p (c t)"), in_=tb(bt)),
              bnames(bt), ["qkT"])
            free(bt)

        def D2(l, t):
            ba = alloc1()
            A("pe", lambda: [PE.matmul(pf[:, ba * 512 + h * 128: ba * 512 + (h + 1) * 128],
                                       qkT[:, 4 + h, :], qkT[:, h, :], start=True, stop=True)
                             for h in range(4)][-1], ["qkT"], bnames(ba))
            A("dve", lambda: V.tensor_tensor(out=attm[:].rearrange("p h i -> p (h i)"), in0=bk(ba),
                                             in1=Dp[:].rearrange("p h i -> p (h i)"), op=ALU.mult),
              bnames(ba) + ["Dp"], ["attm"])
            free(ba)

        def D3(l, t):
            p2 = t % 2
            bo = alloc1()
            st8[(l, t, "bo")] = bo

            def mm_o():
                last = None
                for h in range(4):
                    o = pf[:, bo * 512 + h * 128: bo * 512 + (h + 1) * 128]
                    last = PE.matmul(o, attm[:, h, :], Vret[p2][:, h, :], start=True, stop=(t == 0))
                    if t > 0:
                        last = PE.matmul(o, qkT[:, h, :], Sbf[:, h, :], start=False, stop=True)
                return last
            A("pe", mm_o, ["attm", f"Vret{p2}", "qkT", "Sbf"], bnames(bo))
            bkv = alloc1()
            A("pe", lambda: [PE.matmul(pf[:, bkv * 512 + h * 128: bkv * 512 + (h + 1) * 128],
                                       Kd[:, h, :], Vret[p2][:, h, :], start=True, stop=True)
                             for h in range(4)][-1], ["Kd", f"Vret{p2}"], bnames(bkv))
            if t == 0:
                A("dve", lambda: V.tensor_copy(out=S[:].rearrange("p h e -> p (h e)"), in_=bk(bkv)),
                  bnames(bkv), ["S"])
            else:
                A("dve", lambda: [V.scalar_tensor_tensor(
                    out=S[:, h, :], in0=S[:, h, :], scalar=cst["sdec"][h],
                    in1=pf[:, bkv * 512 + h * 128: bkv * 512 + (h + 1) * 128],
                    op0=ALU.mult, op1=ALU.add) for h in range(4)][-1], ["S"] + bnames(bkv), ["S"])
            free(bkv)
            A("pool", lambda: POOL.tensor_copy(out=Sbf[:], in_=S[:]), ["S"], ["Sbf"])

        def D4(l, t):
            p2 = t % 2
            bo = st8[(l, t, "bo")]
            A("act", lambda: ACT.activation(out=sq[:].rearrange("p h e -> p (h e)"), in_=bk(bo),
                                            func=AF.Square), bnames(bo), ["sq"])
            A("dve", lambda: V.tensor_tensor(out=tmpr[:].rearrange("p h e -> p (h e)"), in0=bk(bo),
                                             in1=sgr[p2][:], op=ALU.mult),
              bnames(bo) + [f"sgr{p2}", "sq"], ["tmpr"])
            free(bo)
            A("dve", lambda: V.tensor_reduce(out=ssr, in_=sq[:], axis=AX.X, op=ALU.add), ["sq"], ["ssr"])
            A("dve", lambda: V.scalar_tensor_tensor(out=ur, in0=ssr, scalar=1.0 / 32.0, in1=epsg[:],
                                                    op0=ALU.mult, op1=ALU.add), ["ssr", "epsg"], ["ur"])
            A("pool", lambda: POOL.tensor_tensor(out=rr, in0=ur, in1=cm05[:], op=ALU.pow),
              ["ur", "cm05"], ["rr"])
            A("pool", lambda: POOL.tensor_tensor(out=ybuf[:, 512:1024].rearrange("p (h e) -> p h e", h=4),
                                                 in0=tmpr[:],
                                                 in1=rr.unsqueeze(2).broadcast_to([128, 4, 128]),
                                                 op=ALU.mult), ["tmpr", "rr"], ["yr"])

        def E1(l, t):
            bt = alloc1()
            A("pe", lambda: [PE.transpose(tb(bt)[:, c * 128:(c + 1) * 128], ybuf[:, c * 128:(c + 1) * 128], ident[:])
                             for c in range(8)][-1], ["ya", "yr", "ident"], bnames(bt))
            A("act", lambda: ACT.copy(out=yT[:].rearrange("p c t -> p (c t)"), in_=tb(bt)), bnames(bt), ["yT"])
            free(bt)

        def E2(l, t, half, part=None):
            xb, xn = xt[t % 3], f"xt{t % 3}"
            if part in (None, "pe"):
                bw = alloc1()
                st8[(l, t, "bw", half)] = bw

                def mm_out():
                    last = None
                    for c in range(8):
                        last = PE.matmul(pf[:, bw * 512:(bw + 1) * 512], yT[:, c, :],
                                         wout[:, c, half * 512:(half + 1) * 512],
                                         start=(c == 0), stop=(c == 7))
                    return last
                A("pe", mm_out, ["yT", "wout"], bnames(bw))
            if part in (None, "add"):
                bw = st8[(l, t, "bw", half)]
                A("dve", lambda: V.tensor_tensor(out=xb[:, half * 512:(half + 1) * 512],
                                                 in0=xb[:, half * 512:(half + 1) * 512], in1=bk(bw), op=ALU.add),
                  [xn] + bnames(bw), [xn])
                free(bw)

        def E3(l, t, last_layer, part=None):
            xb, xn = xt[t % 3], f"xt{t % 3}"
            b = t % 3
            if not last_layer:
                if part in (None, "a"):
                    dma("sp", x1_d[t * 128:(t + 1) * 128, :], xb[:], [xn], [f"x1_{t}"], chan=f"xs{b}")
                return
            if part in (None, "a") and final:
                A("act", lambda: ACT.activation(out=junk[:], in_=xb[:], func=AF.Square, accum_out=ssq2),
                  [xn], ["junk", "ssq2"])
                A("pool", lambda: POOL.tensor_scalar(out=u2, in0=ssq2, scalar1=1.0 / D, scalar2=EPS,
                                                     op0=ALU.mult, op1=ALU.add), ["ssq2"], ["u2"])
                A("pool", lambda: POOL.tensor_tensor(out=rstd2, in0=u2, in1=cm05[:, 0:1], op=ALU.pow),
                  ["u2", "cm05"], ["rstd2"])
            if part in (None, "b"):
                if final:
                    A("dve", lambda: V.scalar_tensor_tensor(out=xb[:], in0=xb[:], scalar=rstd2, in1=fgB[:],
                                                            op0=ALU.mult, op1=ALU.mult),
                      [xn, "rstd2", "fgB"], [xn])
                dma("sp", y_d[t * 128:(t + 1) * 128, :], xb[:], [xn], [f"y_{t}"], chan=f"out{b}")

        for li, l in enumerate(layers):
            if li == 0:
                load_win(l)
            load_wout(l)
            last_layer = (li == len(layers) - 1)
            nl = None if last_layer else layers[li + 1]
            if li > 0:
                load_x(li, 0)
            if nt > 1:
                load_x(li, 1)
            A_norm_a(l, 0)
            A_norm_d(l, 0)
            A_tr(l, 0)
            for g in range(4):
                B(l, 0, g)
            for t in range(nt):
                nxt = t + 1 < nt
                prv = t > 0
                lastt = (t == nt - 1) and (nl is not None)
                if nxt:
                    A_norm_a(l, t + 1)
                if lastt:
                    load_win(nl, range(0, 4))
                B(l, t, 4)
                C_scores2(l, t, 0)
                if nxt:
                    A_norm_d(l, t + 1)
                B(l, t, 5)
                B(l, t, 55)
                if prv:
                    E1(l, t - 1)
                B(l, t, 6)
                C_pv(l, t, 0)
                C_pv(l, t, 1)
                C_scores2(l, t, 2)
                B(l, t, 56)
                if prv:
                    E2(l, t - 1, 0)
                    E2(l, t - 1, 1)
                B(l, t, 57)
                if nxt:
                    A_tr(l, t + 1)
                D1(l, t)
                if prv:
                    E3(l, t - 1, last_layer, "a")
                C_pv(l, t, 2)
                C_pv(l, t, 3)
                C_scores2(l, t, 4)
                B(l, t, 7)
                if lastt:
                    load_win(nl, range(4, 8))
                D2(l, t)
                if nxt:
                    B(l, t + 1, 0)
                if prv:
                    E3(l, t - 1, last_layer, "b")
                if t + 2 < nt:
                    load_x(li, t + 2)
                C_pv(l, t, 4)
                C_pv(l, t, 5)
                C_scores2(l, t, 6)
                D3(l, t)
                D4(l, t)
                if nxt:
                    B(l, t + 1, 1)
                    B(l, t + 1, 2)
                C_pv(l, t, 6)
                C_pv(l, t, 7)
                C_fin(l, t, 0)
                C_fin(l, t, 1)
                C_gate(l, t)
                if nxt:
                    B(l, t + 1, 3)
            E1(l, nt - 1)
            E2(l, nt - 1, 0)
            E2(l, nt - 1, 1)
            E3(l, nt - 1, last_layer)

        sem_keys = ["pe", "act", "dve", "pool", "sp"] + sorted(sch.chans)
        sems = {}
        for k in sem_keys:
            sems[k] = es.enter_context(nc.semaphore(k))
        sch.emit(sems)
    return nc, cst


def make_in_maps(inputs, cst, layers_x=None):
    x = np.ascontiguousarray(inputs["x"], dtype=np.float32)
    rbr = np.ascontiguousarray(inputs["rel_bias"][:, :, ::-1], dtype=np.float32).reshape(16, 513)
    common = {
        "pos": np.ascontiguousarray(inputs["positions"], dtype=np.int32),
        "norm_gain": np.ascontiguousarray(inputs["norm_gain"], dtype=np.float32),
        "w_in": np.ascontiguousarray(inputs["w_in"], dtype=np.float32),
        "w_out": np.ascontiguousarray(inputs["w_out"], dtype=np.float32),
        "rbr": rbr,
        "final_gain": np.ascontiguousarray(inputs["final_gain"], dtype=np.float32),
        "c_ident": cst["ident"], "c_invf": cst["invf"], "c_dp": cst["dp"],
        "c_kdec": cst["kdec"], "c_epsg": cst["epsg"], "c_mask": cst["mask"],
    }
    maps = []
    for c in range(NCORES):
        m = dict(common)
        m["x"] = x[c]
        maps.append(m)
    return maps


_CACHE = {}


def kernel(x, positions, norm_gain, w_in, w_out, rel_bias, final_gain):
    inputs = dict(x=np.asarray(x), positions=np.asarray(positions), norm_gain=np.asarray(norm_gain),
                  w_in=np.asarray(w_in), w_out=np.asarray(w_out), rel_bias=np.asarray(rel_bias),
                  final_gain=np.asarray(final_gain))
    if "nc" not in _CACHE:
        _CACHE["nc"] = build_program()
    nc, cst = _CACHE["nc"]
    maps = make_in_maps(inputs, cst)
    res = run_bass_kernel_spmd(nc, maps, core_ids=list(range(NCORES)))
    out = np.stack([np.asarray(r["y"], dtype=np.float32) for r in res.results], axis=0)
    return out.reshape(8, SEQ, D)
```

```python
import numpy as np
import concourse.bass as bass
import concourse.mybir as mybir
from concourse.bass_utils import run_bass_kernel_spmd

F32 = mybir.dt.float32
BF16 = mybir.dt.bfloat16
I32 = mybir.dt.int32
AF = mybir.ActivationFunctionType
ALU = mybir.AluOpType
AX = mybir.AxisListType

D = 1024
SEQ = 4096
NTILES = 32
NCORES = 8
EPS = 1e-6
TWO_PI = 6.283185307179586
C1 = 6.28125
C2 = TWO_PI - C1
NEG = -30000.0
NSLOT = 6


class _Op:
    __slots__ = ("eng", "fn", "deps", "chan", "signal", "key", "cnt")


class Sched:
    def __init__(self, nc):
        self.nc = nc
        self.engs = {"pe": nc.tensor, "act": nc.scalar, "dve": nc.vector,
                     "pool": nc.gpsimd, "sp": nc.sync}
        self.ops = []
        self.lw = {}
        self.rd = {}
        self.chans = set()

    def add(self, eng, fn, reads=(), writes=(), chan=None):
        op = _Op()
        op.eng, op.fn, op.chan, op.signal = eng, fn, chan, False
        op.key, op.cnt = None, 0
        deps = set()
        for r in reads:
            w = self.lw.get(r)
            if w is not None:
                deps.add(w)
        for wn in writes:
            w = self.lw.get(wn)
            if w is not None:
                deps.add(w)
            for r in self.rd.get(wn, ()):
                deps.add(r)
        op.deps = deps
        for r in reads:
            self.rd.setdefault(r, []).append(op)
        for wn in writes:
            self.lw[wn] = op
            self.rd[wn] = []
        if chan is not None:
            self.chans.add(chan)
        self.ops.append(op)
        return op

    @staticmethod
    def _skip(d, op):
        return d.eng == "pe" and op.eng == "pe" and d.chan is None and op.chan is None

    def emit(self, sems, final_engine="sp"):
        index = {id(op): i for i, op in enumerate(self.ops)}
        for op in self.ops:
            newest = {}
            keep = set()
            for d in op.deps:
                if self._skip(d, op):
                    continue
                if d.chan is None and d.eng in ("pe", "act", "dve"):
                    cur = newest.get(d.eng)
                    if cur is None or index[id(d)] > index[id(cur)]:
                        newest[d.eng] = d
                else:
                    keep.add(d)
            keep.update(newest.values())
            for d in keep:
                d.signal = True
            op.deps = keep
        counters = {}
        waited = {e: {} for e in self.engs}
        last_out = {}
        for op in self.ops:
            engine = self.engs[op.eng]
            need = {}
            for d in op.deps:
                if self._skip(d, op):
                    continue
                if need.get(d.key, 0) < d.cnt:
                    need[d.key] = d.cnt
            for key, val in need.items():
                if waited[op.eng].get(key, 0) >= val:
                    continue
                engine.wait_ge(sems[key], val)
                waited[op.eng][key] = val
            inst = op.fn()
            if op.signal or op.chan is not None:
                if op.chan is not None:
                    key, inc = op.chan, 16
                else:
                    key, inc = op.eng, 1
                counters[key] = counters.get(key, 0) + inc
                inst.then_inc(sems[key], inc)
                op.key, op.cnt = key, counters[key]
                if op.chan is not None and op.chan.startswith("out"):
                    last_out[key] = counters[key]
        fe = self.engs[final_engine]
        for key, val in last_out.items():
            fe.wait_ge(sems[key], val)


def _rev_last(ap):
    pat = [list(x) for x in ap.ap]
    step, n = pat[-1]
    pat[-1] = [-step, n]
    return bass.AP(ap.tensor, ap.offset + (n - 1) * step, pat)


def _consts():
    c = {}
    c["ident"] = np.eye(128, dtype=np.float32)
    c["invf"] = (np.float32(1.0) / np.power(np.float32(10000.0),
                 np.linspace(0.0, 1.0, 64, dtype=np.float32))).astype(np.float32)
    h = np.arange(4, dtype=np.float64)
    lg = np.log1p(-np.exp2(-5.0 - h))
    s = 128.0 ** -0.5
    i = np.arange(128, dtype=np.float64)
    j = np.arange(128, dtype=np.float64)
    dp = np.zeros((128, 4, 128), np.float64)
    for hh in range(4):
        same = (j[:, None] // 64) == (i[None, :] // 64)
        cross = (j[:, None] < 64) & (i[None, :] >= 64)
        dist = np.where(same, np.abs(i[None, :] - j[:, None]), i[None, :] - j[:, None])
        val = np.exp(lg[hh] * (dist - (i[None, :] + 1.0))) * s
        dp[:, hh, :] = np.where(same | cross, val, 0.0)
    c["dp"] = dp.astype(np.float32)
    c["kdec"] = (np.exp(lg[None, :] * (127.0 - j[:, None])) * s).astype(np.float32)
    g2 = np.exp(2.0 * lg[None, :] * (i[:, None] + 1.0))
    c["epsg"] = (4.0 * EPS / g2).astype(np.float32)
    c["sdec"] = [float(np.exp(128.0 * lg[hh])) for hh in range(4)]
    m = np.zeros((128, 5, 128), np.float32)
    m[64:128, 4, 0:64] = NEG
    c["mask"] = m
    return c


def build_program(layers=(0, 1), nt=NTILES, final=True, x_in_name="x", dbg=False):
    nc = bass.Bass("TRN2", target_bir_lowering=False, dynamic_dma_scratch_size=16384)
    cst = _consts()
    ntok = nt * 128
    x_d = nc.dram_tensor("x", [ntok, D], F32, kind="ExternalInput").ap()
    pos_d = nc.dram_tensor("pos", [ntok], I32, kind="ExternalInput").ap()
    ng_d = nc.dram_tensor("norm_gain", [2, D], F32, kind="ExternalInput").ap()
    win_d = nc.dram_tensor("w_in", [2, D, 4096], F32, kind="ExternalInput").ap()
    wout_d = nc.dram_tensor("w_out", [2, D, D], F32, kind="ExternalInput").ap()
    rbr_d = nc.dram_tensor("rbr", [16, 513], F32, kind="ExternalInput").ap()
    fg_d = nc.dram_tensor("final_gain", [D], F32, kind="ExternalInput").ap()
    ident_d = nc.dram_tensor("c_ident", [128, 128], F32, kind="ExternalInput").ap()
    invf_d = nc.dram_tensor("c_invf", [64], F32, kind="ExternalInput").ap()
    dp_d = nc.dram_tensor("c_dp", [128, 4, 128], F32, kind="ExternalInput").ap()
    kdec_d = nc.dram_tensor("c_kdec", [128, 4], F32, kind="ExternalInput").ap()
    epsg_d = nc.dram_tensor("c_epsg", [128, 4], F32, kind="ExternalInput").ap()
    mask_d = nc.dram_tensor("c_mask", [128, 5, 128], F32, kind="ExternalInput").ap()
    y_d = nc.dram_tensor("y", [ntok, D], F32, kind="ExternalOutput").ap()
    x1_t = nc.dram_tensor("x1s", [ntok, D], F32, kind="Internal")
    x1_d = x1_t.ap()
    gr_t = nc.dram_tensor("grs", [16, 768], F32, kind="Internal")
    gr_d = gr_t.ap()

    from contextlib import ExitStack
    es = ExitStack()

    def sb(name, shape, dt):
        return es.enter_context(nc.sbuf_tensor(name, shape, dt))

    with es:
        win = sb("win", [128, 8, 4096], BF16)
        wout = sb("wout", [128, 8, 1024], BF16)
        xt = [sb(f"xt{i}", [128, 1024], F32) for i in range(3)]
        hn = sb("hn", [128, 1024], BF16)
        junk = sb("junk", [128, 1024], BF16)
        hnT = [sb(f"hnT{i}", [128, 8, 128], BF16) for i in range(2)]
        gB = [sb(f"gB{l}", [128, 1024], F32) for l in range(2)]
        fgB = sb("fgB", [128, 1024], F32)
        qTa = [sb(f"qTa{i}", [128, 4, 128], BF16) for i in range(2)]
        kTr = sb("kTr", [128, 4, NSLOT, 128], BF16)
        Vr = sb("Vr", [128, NSLOT, 8, 65], BF16)
        sga = [sb(f"sga{i}", [128, 512], F32) for i in range(2)]
        sgr = [sb(f"sgr{i}", [128, 512], F32) for i in range(2)]
        btab = [sb(f"btab{l}", [128, 8, 5, 128], BF16) for l in range(2)]
        PT = [sb(f"PT{i}", [128, 5, 128], BF16) for i in range(4)]
        cs = sb("cs", [128, 2, 32, 64], F32)
        cosT = cs[:, 0, 0:nt, :]
        sinT = cs[:, 1, 0:nt, :]
        mb = sb("mb", [128, 8, 2, 64], F32)
        qkrot = [sb(f"qkrot{i}", [128, 8, 128], BF16) for i in range(2)]
        qkT = sb("qkT", [128, 8, 128], BF16)
        Kd = sb("Kd", [128, 4, 128], BF16)
        Vret = [sb(f"Vret{i}", [128, 4, 128], BF16) for i in range(2)]
        Dp = sb("Dp", [128, 4, 128], F32)
        attm = sb("attm", [128, 4, 128], BF16)
        S = sb("S", [128, 4, 128], F32)
        Sbf = sb("Sbf", [128, 4, 128], BF16)
        sq = sb("sq", [128, 4, 128], F32)
        ybuf = sb("ybuf", [128, 1024], BF16)
        yT = sb("yT", [128, 8, 128], BF16)
        tmpa = sb("tmpa", [128, 4 * 128], F32)
        tmpr = sb("tmpr", [128, 4, 128], F32)
        identf = sb("identf", [128, 128], F32)
        ident = sb("ident", [128, 128], BF16)
        small = sb("small", [128, 64], F32)
        kdec = sb("kdec", [128, 4], F32)
        epsg = sb("epsg", [128, 4], F32)
        cm05 = sb("cm05", [128, 4], F32)
        mcol = sb("mcol", [128, 1], F32)
        posi = sb("posi", [nt, 128], I32)
        posr = sb("posr", [nt, 128], F32)
        posf = sb("posf", [128, nt], F32)
        invfB = sb("invfB", [128, 64], F32)
        wv = wout[:].rearrange("p c f -> p (c f)").bitcast(F32)
        nF = wv[:, 0:nt * 64].rearrange("p (t f) -> p t f", f=64)
        nI = wv[:, 2048:2048 + nt * 64].bitcast(I32).rearrange("p (t f) -> p t f", f=64)
        stage = [xt[1][:, 0:384].rearrange("p (c q) -> p c q", q=128),
                 xt[1][:, 384:768].rearrange("p (c q) -> p c q", q=128),
                 xt[2][:, 0:384].rearrange("p (c q) -> p c q", q=128),
                 xt[2][:, 384:768].rearrange("p (c q) -> p c q", q=128)]
        stgn = ["xt1", "xt1", "xt2", "xt2"]
        rb = tmpa[0:16, 0:384]
        zer = tmpr[:].rearrange("p h e -> p (h e)")[0:16, 0:383]
        maskT = mb[:].rearrange("p h two f -> p (h two f)")[:, 0:384].rearrange("p (c q) -> p c q", q=128)
        ch0 = small[0:16, 32:33]
        ssq = small[:, 0:1]
        u0 = small[:, 1:2]
        rstd = small[:, 2:3]
        ssq2 = small[:, 3:4]
        u2 = small[:, 4:5]
        rstd2 = small[:, 5:6]
        rden = small[:, 8:16]
        ssr = small[:, 16:20]
        ur = small[:, 20:24]
        rr = small[:, 24:28]

        pf = es.enter_context(nc.psum_tensor("pf", [128, 8 * 512], F32))

        def tb(b):
            return pf[:, b * 512:(b + 1) * 512].bitcast(BF16)

        sch = Sched(nc)
        A = sch.add
        V, ACT, PE, POOL, SP = nc.vector, nc.scalar, nc.tensor, nc.gpsimd, nc.sync

        GEN = [4, 5, 6, 7]
        busy = set()
        freed_at = {b: -1 for b in GEN}
        clock = [0]

        def alloc1():
            cand = [b for b in GEN if b not in busy]
            if not cand:
                raise RuntimeError("no free PSUM bank (schedule keeps too many live)")
            b = min(cand, key=lambda x: freed_at[x])
            busy.add(b)
            return b

        def alloc2():
            cand = [b for b in GEN[:-1] if b not in busy and (b + 1) not in busy]
            if not cand:
                raise RuntimeError("no free PSUM bank pair")
            b = min(cand, key=lambda x: max(freed_at[x], freed_at[x + 1]))
            busy.add(b)
            busy.add(b + 1)
            return b

        def free(b, n=1):
            clock[0] += 1
            for i in range(n):
                busy.discard(b + i)
                freed_at[b + i] = clock[0]

        def bk(b, n=1):
            return pf[:, b * 512:(b + n) * 512]

        def bnames(b, n=1):
            return [f"pf{b + i}" for i in range(n)]

        dma_id = [0]

        def dma(eng, out, in_, reads, writes, chan=None, **kw):
            if chan is None:
                chan = f"d{dma_id[0]}"
                dma_id[0] += 1
            e = {"sp": SP, "pool": POOL, "act": ACT}[eng]
            return A(eng, lambda: e.dma_start(out=out, in_=in_, **kw), reads, writes, chan=chan)

        dma("sp", identf[:], ident_d, [], ["identf"])
        dma("sp", xt[0][:], x_d[0:128, :], [], ["xt0"], chan="xl0")
        dma("sp", rb, rbr_d[:, 0:384], ["tmpa0", "tmpa1"], ["rb"])
        dma("sp", maskT, mask_d[:, 2:5, :], ["mb"], ["maskT"])
        dma("sp", posi[:], pos_d.rearrange("(t p) -> t p", p=128), [], ["posi"])
        dma("sp", invfB[:], invf_d.partition_broadcast(128), [], ["invfB"])
        dma("sp", Dp[:], dp_d, [], ["Dp"])
        dma("sp", kdec[:], kdec_d, [], ["kdec"])
        dma("sp", epsg[:], epsg_d, [], ["epsg"])
        for l in range(2):
            dma("sp", gB[l][:], ng_d[l].partition_broadcast(128), [], [f"gB{l}"])
        dma("sp", fgB[:], fg_d.partition_broadcast(128), [], ["fgB"])

        A("dve", lambda: V.tensor_copy(out=ident[:], in_=identf[:]), ["identf"], ["ident"])
        A("pool", lambda: POOL.memset(cm05[:], -0.5), [], ["cm05"])
        A("pool", lambda: [POOL.memset(mcol[64:128, :], 0.0), POOL.memset(mcol[0:64, :], NEG)][-1], [], ["mcol"])
        A("pool", lambda: POOL.memset(Vr[:], 2.0), [], [f"V{s}" for s in range(NSLOT)])

        A("dve", lambda: V.memset(zer, 0.0), ["tmpr"], ["zer"])
        A("dve", lambda: V.tensor_copy(out=ch0, in_=rb[:, 0:1]), ["rb"], ["ch0"])
        A("dve", lambda: V.tensor_scalar(out=rb, in0=rb, scalar1=ch0, scalar2=None, op0=ALU.subtract),
          ["rb", "ch0", "tmpa0", "tmpa1"], ["rb"])
        dma("sp", gr_d[:, 0:383], zer, ["zer", "tmpr"], ["gr0"])
        dma("sp", gr_d[:, 383:767], rb, ["rb", "tmpa0", "tmpa1"], ["gr1"])
        NST = 4

        def bias_dma(lh):
            src = bass.AP(gr_t, lh * 768 + 256, [[1, 128], [128, 3], [1, 128]])
            dma("sp", stage[lh % NST], src, ["gr0", "gr1", stgn[lh % NST]], [f"stage{lh % NST}"],
                chan=f"stg{lh % NST}")

        def bias_add(lh):
            l_, h_ = lh // 8, lh % 8
            st = stage[lh % NST]
            A("dve", lambda: V.tensor_tensor(out=btab[l_][:, h_, 2:5, :], in0=_rev_last(st), in1=maskT, op=ALU.add),
              [f"stage{lh % NST}", "maskT", "mb", stgn[lh % NST]], [f"btab{l_}"])

        bias_state = [0]
        for k0 in range(NST):
            bias_dma(k0)

        def bias_step():
            k = bias_state[0]
            if k < 16:
                bias_add(k)
                if k + NST < 16:
                    bias_dma(k + NST)
                bias_state[0] += 1

        def AD(fn, reads, writes):
            A("dve", fn, reads, writes)
            bias_step()

        A("dve", lambda: V.tensor_copy(out=posr[:], in_=posi[:]), ["posi"], ["posr"])
        A("pe", lambda: PE.transpose(pf[:, 0:nt], posr[:], identf[0:nt, 0:nt]), ["posr", "identf"], ["pf0"])
        A("dve", lambda: V.tensor_copy(out=posf[:], in_=pf[:, 0:nt]), ["pf0"], ["posf"])

        AD(lambda: V.tensor_tensor(out=cosT, in0=posf[:].unsqueeze(2).broadcast_to([128, nt, 64]),
                                   in1=invfB[:].unsqueeze(1).broadcast_to([128, nt, 64]),
                                   op=ALU.mult), ["posf", "invfB"], ["cosT"])

        def sin_table(dst, src, shift, name, srcname):
            if shift != 0.0:
                AD(lambda: V.tensor_scalar(out=src, in0=src, scalar1=float(shift), scalar2=None,
                                           op0=ALU.add), [srcname], [srcname])
            AD(lambda: V.tensor_scalar(out=nF, in0=src, scalar1=1.0 / TWO_PI, scalar2=None,
                                       op0=ALU.mult), [srcname], ["nF", "wout"])
            AD(lambda: V.tensor_copy(out=nI, in_=nF), ["nF"], ["nI", "wout"])
            AD(lambda: V.tensor_copy(out=nF, in_=nI), ["nI"], ["nF", "wout"])
            AD(lambda: V.scalar_tensor_tensor(out=dst, in0=nF, scalar=-C1, in1=src,
                                              op0=ALU.mult, op1=ALU.add), ["nF", srcname, "wout"], [name])
            AD(lambda: V.scalar_tensor_tensor(out=dst, in0=nF, scalar=-C2, in1=dst,
                                              op0=ALU.mult, op1=ALU.add), ["nF", name, "wout"], [name])
            AD(lambda: V.tensor_scalar(out=dst, in0=dst, scalar1=3.1415925, scalar2=-3.1415925,
                                       op0=ALU.min, op1=ALU.max), [name], [name])
            A("act", lambda: ACT.activation(out=dst, in_=dst, func=AF.Sin), [name], [name])

        sin_table(sinT, cosT, 0.0, "sinT", "cosT")
        sin_table(cosT, cosT, np.pi / 2, "cosT", "cosT")
        while bias_state[0] < 16:
            bias_step()

        def load_wout(l):
            dma("pool", wout[:], wout_d[l].rearrange("(c p) f -> p c f", p=128), [], ["wout"],
                chan="wo", max_dma_last_dim=8192)

        def load_win(l, groups=range(8)):
            src = win_d[l].rearrange("(c p) f -> p c f", p=128)
            for g in groups:
                dma("pool", win[:, :, g * 512:(g + 1) * 512], src[:, :, g * 512:(g + 1) * 512],
                    [], [f"win{g}"], chan=f"wi{g}", max_dma_last_dim=8192)

        st8 = {}

        def load_x(li, t):
            b = t % 3
            src = (x_d if li == 0 else x1_d)[t * 128:(t + 1) * 128, :]
            rds = [] if li == 0 else [f"x1_{t}"]
            dma("sp", xt[b][:], src, rds, [f"xt{b}"], chan=f"xl{b}")

        def A_norm_a(l, t):
            xb, xn = xt[t % 3], f"xt{t % 3}"
            A("act", lambda: ACT.activation(out=junk[:], in_=xb[:], func=AF.Square, accum_out=ssq),
              [xn], ["junk", "ssq"])
            A("pool", lambda: POOL.tensor_scalar(out=u0, in0=ssq, scalar1=1.0 / D, scalar2=EPS,
                                                 op0=ALU.mult, op1=ALU.add), ["ssq"], ["u0"])
            A("pool", lambda: POOL.tensor_tensor(out=rstd, in0=u0, in1=cm05[:, 0:1], op=ALU.pow),
              ["u0", "cm05"], ["rstd"])

        def A_norm_d(l, t):
            xb, xn = xt[t % 3], f"xt{t % 3}"
            A("dve", lambda: V.scalar_tensor_tensor(out=hn[:], in0=xb[:], scalar=rstd, in1=gB[l][:],
                                                    op0=ALU.mult, op1=ALU.mult),
              [xn, "rstd", f"gB{l}"], ["hn"])

        def A_tr(l, t):
            p2 = t % 2
            bt = alloc1()
            A("pe", lambda: [PE.transpose(tb(bt)[:, c * 128:(c + 1) * 128], hn[:, c * 128:(c + 1) * 128], ident[:])
                             for c in range(8)][-1], ["hn", "ident"], bnames(bt))
            A("act", lambda: ACT.copy(out=hnT[p2][:].rearrange("p c t -> p (c t)"), in_=tb(bt)),
              bnames(bt), [f"hnT{p2}"])
            free(bt)

        def mm_tm(b0, g, hT):
            last = None
            for c in range(8):
                last = PE.matmul(pf[:, b0 * 512:(b0 + 1) * 512], hT[:, c, :],
                                 win[:, c, g * 512:(g + 1) * 512], start=(c == 0), stop=(c == 7))
            return last

        def B(l, t, g):
            p2 = t % 2
            slot = t % NSLOT
            hT, hTn = hnT[p2], f"hnT{p2}"
            if g in (0, 1):
                bq = alloc1()

                def mm_fm():
                    last = None
                    for fc in range(4):
                        f0 = (g * 4 + fc) * 128
                        for c in range(8):
                            last = PE.matmul(pf[:, bq * 512 + fc * 128: bq * 512 + (fc + 1) * 128],
                                             win[:, c, f0:f0 + 128], hT[:, c, :],
                                             start=(c == 0), stop=(c == 7))
                    return last
                A("pe", mm_fm, [hTn, f"win{g}"], bnames(bq))
                if g == 0:
                    A("act", lambda: ACT.mul(out=qTa[p2][:].rearrange("p a t -> p (a t)"),
                                             in_=bk(bq), mul=0.125), bnames(bq), [f"qTa{p2}"])
                else:
                    A("act", lambda: ACT.copy(out=kTr[:, :, slot, :],
                                              in_=bk(bq).rearrange("p (a t) -> p a t", a=4)),
                      bnames(bq), [f"kT{slot}"])
                free(bq)
            elif g == 2:
                bv = alloc1()
                A("pe", lambda: mm_tm(bv, 2, hT), [hTn, "win2"], bnames(bv))
                A("dve", lambda: V.tensor_copy(out=Vr[:, slot, :, 0:64],
                                               in_=bk(bv).rearrange("p (h e) -> p h e", h=8)),
                  bnames(bv), [f"V{slot}"])
                free(bv)
            elif g in (3, 7):
                bg = alloc1()
                sg = (sga if g == 3 else sgr)[p2]
                sgn = ("sga" if g == 3 else "sgr") + str(p2)
                A("pe", lambda: mm_tm(bg, g, hT), [hTn, f"win{g}"], bnames(bg))
                A("act", lambda: ACT.activation(out=sg[:], in_=bk(bg), func=AF.Tanh, scale=0.5),
                  bnames(bg), [sgn])
                A("dve", lambda: V.scalar_tensor_tensor(out=sg[:], in0=sg[:], scalar=1.0, in1=bk(bg),
                                                        op0=ALU.add, op1=ALU.mult),
                  [sgn] + bnames(bg), [sgn])
                free(bg)
            elif g == 4:
                br = alloc2()
                st8[(l, t, "br")] = br
                A("pe", lambda: mm_tm(br, 4, hT), [hTn, "win4"], bnames(br))
            elif g == 5:
                br = st8[(l, t, "br")]
                A("pe", lambda: mm_tm(br + 1, 5, hT), [hTn, "win5"], bnames(br + 1))
            elif g == 55:
                br = st8[(l, t, "br")]
                v4 = bk(br, 2).rearrange("p (h two f) -> p h two f", h=8, two=2)
                pat = [list(x) for x in v4.ap]
                pat[2] = [-pat[2][0], 2]
                v4s = bass.AP(v4.tensor, v4.offset + 64, pat)
                A("dve", lambda: V.tensor_tensor(
                    out=mb[:], in0=v4s,
                    in1=sinT[:, t, :].unsqueeze(1).unsqueeze(1).broadcast_to([128, 8, 2, 64]),
                    op=ALU.mult), bnames(br, 2) + ["sinT"], ["mb"])
                A("dve", lambda: V.tensor_tensor(
                    out=v4, in0=v4,
                    in1=cosT[:, t, :].unsqueeze(1).unsqueeze(1).broadcast_to([128, 8, 2, 64]),
                    op=ALU.mult), bnames(br, 2) + ["cosT"], bnames(br, 2))
            elif g == 56:
                br = st8[(l, t, "br")]
                v4 = bk(br, 2).rearrange("p (h two f) -> p h two f", h=8, two=2)
                qk4 = qkrot[p2][:].rearrange("p h (two f) -> p h two f", two=2)
                A("dve", lambda: V.tensor_tensor(out=qk4[:, :, 0, :], in0=v4[:, :, 0, :], in1=mb[:, :, 0, :],
                                                 op=ALU.subtract), bnames(br, 2) + ["mb"], [f"qkrot{p2}"])
                A("dve", lambda: V.tensor_tensor(out=qk4[:, :, 1, :], in0=v4[:, :, 1, :], in1=mb[:, :, 1, :],
                                                 op=ALU.add), bnames(br, 2) + ["mb"], [f"qkrot{p2}"])
            elif g == 57:
                free(st8[(l, t, "br")], 2)
                A("pool", lambda: POOL.tensor_tensor(out=Kd[:], in0=qkrot[p2][:, 4:8, :],
                                                     in1=kdec[:].unsqueeze(2).broadcast_to([128, 4, 128]),
                                                     op=ALU.mult), [f"qkrot{p2}", "kdec"], ["Kd"])
            elif g == 6:
                bv2 = alloc1()
                A("pe", lambda: mm_tm(bv2, 6, hT), [hTn, "win6"], bnames(bv2))
                A("act", lambda: ACT.copy(out=Vret[p2][:].rearrange("p h e -> p (h e)"), in_=bk(bv2)),
                  bnames(bv2), [f"Vret{p2}"])
                free(bv2)

        def att_info(t):
            nk = min(t, 4) + 1
            c0 = 5 - nk
            chunks = list(range(c0, 5))
            kslot = {c: (t - 4 + c) % NSLOT for c in chunks}
            return c0, chunks, kslot

        def C_scores(l, t, h):
            p2 = t % 2
            c0, chunks, kslot = att_info(t)
            hp, e = h // 2, h % 2
            r0 = 64 * e
            BS = 2 * e
            ptb, ptn = PT[h % 4], f"PT{h % 4}"

            def mm_sc():
                last = None
                for c in chunks:
                    last = PE.matmul(pf[:, BS * 512 + c * 128: BS * 512 + (c + 1) * 128],
                                     kTr[r0:r0 + 64, hp, kslot[c], :], qTa[p2][r0:r0 + 64, hp, :],
                                     start=True, stop=True)
                return last
            A("pe", mm_sc, [f"qTa{p2}"] + [f"kT{kslot[c]}" for c in chunks], bnames(BS, 2))
            cb = max(c0, 2)

            def bias_ops():
                last = V.tensor_tensor(
                    out=pf[:, BS * 512 + cb * 128: BS * 512 + 640],
                    in0=pf[:, BS * 512 + cb * 128: BS * 512 + 640],
                    in1=btab[l][:, h, cb:5, :].rearrange("p c q -> p (c q)"), op=ALU.add)
                if c0 == 0:
                    last = V.tensor_scalar(out=pf[:, BS * 512 + 64: BS * 512 + 128],
                                           in0=pf[:, BS * 512 + 64: BS * 512 + 128],
                                           scalar1=mcol[:, 0:1], scalar2=None, op0=ALU.add)
                return last
            A("dve", bias_ops, bnames(BS, 2) + [f"btab{l}", "mcol"], bnames(BS, 2))
            A("act", lambda: ACT.activation(
                out=ptb[:, c0:5, :].rearrange("p c q -> p (c q)"),
                in_=pf[:, BS * 512 + c0 * 128: BS * 512 + 640], func=AF.Exp),
              bnames(BS, 2), [ptn])

        def C_scores2(l, t, h0):
            p2 = t % 2
            c0, chunks, kslot = att_info(t)
            hp = h0 // 2

            for e in range(2):
                def mm_sc(e=e):
                    last = None
                    r0 = 64 * e
                    BS = 2 * e
                    for c in chunks:
                        last = PE.matmul(pf[:, BS * 512 + c * 128: BS * 512 + (c + 1) * 128],
                                         kTr[r0:r0 + 64, hp, kslot[c], :], qTa[p2][r0:r0 + 64, hp, :],
                                         start=True, stop=True)
                    return last
                A("pe", mm_sc, [f"qTa{p2}"] + [f"kT{kslot[c]}" for c in chunks], bnames(2 * e, 2))
            cb = max(c0, 2)
            for e in range(2):
                h = h0 + e
                BS = 2 * e
                ptb, ptn = PT[h % 4], f"PT{h % 4}"

                def bias_ops(BS=BS, h=h):
                    last = V.tensor_tensor(
                        out=pf[:, BS * 512 + cb * 128: BS * 512 + 640],
                        in0=pf[:, BS * 512 + cb * 128: BS * 512 + 640],
                        in1=btab[l][:, h, cb:5, :].rearrange("p c q -> p (c q)"), op=ALU.add)
                    if c0 == 0:
                        last = V.tensor_scalar(out=pf[:, BS * 512 + 64: BS * 512 + 128],
                                               in0=pf[:, BS * 512 + 64: BS * 512 + 128],
                                               scalar1=mcol[:, 0:1], scalar2=None, op0=ALU.add)
                    return last
                A("dve", bias_ops, bnames(BS, 2) + [f"btab{l}", "mcol"], bnames(BS, 2))
            for e in range(2):
                h = h0 + e
                BS = 2 * e
                ptb, ptn = PT[h % 4], f"PT{h % 4}"
                A("act", lambda BS=BS, ptb=ptb: ACT.activation(
                    out=ptb[:, c0:5, :].rearrange("p c q -> p (c q)"),
                    in_=pf[:, BS * 512 + c0 * 128: BS * 512 + 640], func=AF.Exp),
                  bnames(BS, 2), [ptn])

        def C_pv(l, t, h):
            c0, chunks, kslot = att_info(t)
            ptb, ptn = PT[h % 4], f"PT{h % 4}"
            BS = 2 * (h % 2)

            def mm_pv():
                last = None
                o0 = BS * 512 + 640 + (h // 2) * 65
                for c in chunks:
                    last = PE.matmul(pf[:, o0:o0 + 65], ptb[:, c, :], Vr[:, kslot[c], h, :],
                                     start=(c == c0), stop=(c == 4))
                return last
            A("pe", mm_pv, [ptn] + [f"V{kslot[c]}" for c in chunks], [f"pf{BS + 1}"])

        def C_fin(l, t, e):
            tmpa4 = tmpa[:].rearrange("p (j e d) -> p j e d", e=2, d=64)
            o0 = 2 * e * 512 + 640
            pv = pf[:, o0:o0 + 260].rearrange("p (h e) -> p h e", e=65)
            A("dve", lambda: V.reciprocal(out=rden[:, e * 4:(e + 1) * 4].unsqueeze(2), in_=pv[:, :, 64:65]),
              ["pf1", "pf3"], [f"rden{e}"])
            A("dve", lambda: V.tensor_tensor(
                out=tmpa4[:, :, e, :], in0=pv[:, :, 0:64],
                in1=rden[:, e * 4:(e + 1) * 4].unsqueeze(2).broadcast_to([128, 4, 64]),
                op=ALU.mult), [f"pf{2 * e + 1}", f"rden{e}"], [f"tmpa{e}"])

        def C_gate(l, t):
            p2 = t % 2
            A("pool", lambda: POOL.tensor_tensor(out=ybuf[:, 0:512], in0=tmpa[:], in1=sga[p2][:], op=ALU.mult),
              ["tmpa0", "tmpa1", f"sga{p2}"], ["ya"])

        def D1(l, t):
            p2 = t % 2
            bt = alloc1()
            A("pe", lambda: [PE.transpose(tb(bt)[:, c * 128:(c + 1) * 128], qkrot[p2][:, c, :], ident[:])
                             for c in range(8)][-1], [f"qkrot{p2}", "ident"], bnames(bt))
            A("act", lambda: ACT.copy(out=qkT[:].rearrange("p c t -> p (c t)"), in_=tb(bt)),
              bnames(bt), ["qkT"])
            free(bt)

        def D2(l, t):
            ba = alloc1()
            A("pe", lambda: [PE.matmul(pf[:, ba * 512 + h * 128: ba * 512 + (h + 1) * 128],
                                       qkT[:, 4 + h, :], qkT[:, h, :], start=True, stop=True)
                             for h in range(4)][-1], ["qkT"], bnames(ba))
            A("dve", lambda: V.tensor_tensor(out=attm[:].rearrange("p h i -> p (h i)"), in0=bk(ba),
                                             in1=Dp[:].rearrange("p h i -> p (h i)"), op=ALU.mult),
              bnames(ba) + ["Dp"], ["attm"])
            free(ba)

        def D3(l, t):
            p2 = t % 2
            bo = alloc1()
            st8[(l, t, "bo")] = bo

            def mm_o():
                last = None
                for h in range(4):
                    o = pf[:, bo * 512 + h * 128: bo * 512 + (h + 1) * 128]
                    last = PE.matmul(o, attm[:, h, :], Vret[p2][:, h, :], start=True, stop=(t == 0))
                    if t > 0:
                        last = PE.matmul(o, qkT[:, h, :], Sbf[:, h, :], start=False, stop=True)
                return last
            A("pe", mm_o, ["attm", f"Vret{p2}", "qkT", "Sbf"], bnames(bo))
            bkv = alloc1()
            A("pe", lambda: [PE.matmul(pf[:, bkv * 512 + h * 128: bkv * 512 + (h + 1) * 128],
                                       Kd[:, h, :], Vret[p2][:, h, :], start=True, stop=True)
                             for h in range(4)][-1], ["Kd", f"Vret{p2}"], bnames(bkv))
            if t == 0:
                A("dve", lambda: V.tensor_copy(out=S[:].rearrange("p h e -> p (h e)"), in_=bk(bkv)),
                  bnames(bkv), ["S"])
            else:
                A("dve", lambda: [V.scalar_tensor_tensor(
                    out=S[:, h, :], in0=S[:, h, :], scalar=cst["sdec"][h],
                    in1=pf[:, bkv * 512 + h * 128: bkv * 512 + (h + 1) * 128],
                    op0=ALU.mult, op1=ALU.add) for h in range(4)][-1], ["S"] + bnames(bkv), ["S"])
            free(bkv)
            A("pool", lambda: POOL.tensor_copy(out=Sbf[:], in_=S[:]), ["S"], ["Sbf"])

        def D4(l, t):
            p2 = t % 2
            bo = st8[(l, t, "bo")]
            A("act", lambda: ACT.activation(out=sq[:].rearrange("p h e -> p (h e)"), in_=bk(bo),
                                            func=AF.Square), bnames(bo), ["sq"])
            A("dve", lambda: V.tensor_tensor(out=tmpr[:].rearrange("p h e -> p (h e)"), in0=bk(bo),
                                             in1=sgr[p2][:], op=ALU.mult),
              bnames(bo) + [f"sgr{p2}", "sq"], ["tmpr"])
            free(bo)
            A("dve", lambda: V.tensor_reduce(out=ssr, in_=sq[:], axis=AX.X, op=ALU.add), ["sq"], ["ssr"])
            A("dve", lambda: V.scalar_tensor_tensor(out=ur, in0=ssr, scalar=1.0 / 32.0, in1=epsg[:],
                                                    op0=ALU.mult, op1=ALU.add), ["ssr", "epsg"], ["ur"])
            A("pool", lambda: POOL.tensor_tensor(out=rr, in0=ur, in1=cm05[:], op=ALU.pow),
              ["ur", "cm05"], ["rr"])
            A("pool", lambda: POOL.tensor_tensor(out=ybuf[:, 512:1024].rearrange("p (h e) -> p h e", h=4),
                                                 in0=tmpr[:],
                                                 in1=rr.unsqueeze(2).broadcast_to([128, 4, 128]),
                                                 op=ALU.mult), ["tmpr", "rr"], ["yr"])

        def E1(l, t):
            bt = alloc1()
            A("pe", lambda: [PE.transpose(tb(bt)[:, c * 128:(c + 1) * 128], ybuf[:, c * 128:(c + 1) * 128], ident[:])
                             for c in range(8)][-1], ["ya", "yr", "ident"], bnames(bt))
            A("act", lambda: ACT.copy(out=yT[:].rearrange("p c t -> p (c t)"), in_=tb(bt)), bnames(bt), ["yT"])
            free(bt)

        def E2(l, t, half, part=None):
            xb, xn = xt[t % 3], f"xt{t % 3}"
            if part in (None, "pe"):
                bw = alloc1()
                st8[(l, t, "bw", half)] = bw

                def mm_out():
                    last = None
                    for c in range(8):
                        last = PE.matmul(pf[:, bw * 512:(bw + 1) * 512], yT[:, c, :],
                                         wout[:, c, half * 512:(half + 1) * 512],
                                         start=(c == 0), stop=(c == 7))
                    return last
                A("pe", mm_out, ["yT", "wout"], bnames(bw))
            if part in (None, "add"):
                bw = st8[(l, t, "bw", half)]
                A("dve", lambda: V.tensor_tensor(out=xb[:, half * 512:(half + 1) * 512],
                                                 in0=xb[:, half * 512:(half + 1) * 512], in1=bk(bw), op=ALU.add),
                  [xn] + bnames(bw), [xn])
                free(bw)

        def E3(l, t, last_layer, part=None):
            xb, xn = xt[t % 3], f"xt{t % 3}"
            b = t % 3
            if not last_layer:
                if part in (None, "a"):
                    dma("sp", x1_d[t * 128:(t + 1) * 128, :], xb[:], [xn], [f"x1_{t}"], chan=f"xs{b}")
                return
            if part in (None, "a") and final:
                A("act", lambda: ACT.activation(out=junk[:], in_=xb[:], func=AF.Square, accum_out=ssq2),
                  [xn], ["junk", "ssq2"])
                A("pool", lambda: POOL.tensor_scalar(out=u2, in0=ssq2, scalar1=1.0 / D, scalar2=EPS,
                                                     op0=ALU.mult, op1=ALU.add), ["ssq2"], ["u2"])
                A("pool", lambda: POOL.tensor_tensor(out=rstd2, in0=u2, in1=cm05[:, 0:1], op=ALU.pow),
                  ["u2", "cm05"], ["rstd2"])
            if part in (None, "b"):
                if final:
                    A("dve", lambda: V.scalar_tensor_tensor(out=xb[:], in0=xb[:], scalar=rstd2, in1=fgB[:],
                                                            op0=ALU.mult, op1=ALU.mult),
                      [xn, "rstd2", "fgB"], [xn])
                dma("sp", y_d[t * 128:(t + 1) * 128, :], xb[:], [xn], [f"y_{t}"], chan=f"out{b}")

        for li, l in enumerate(layers):
            if li == 0:
                load_win(l)
            load_wout(l)
            last_layer = (li == len(layers) - 1)
            nl = None if last_layer else layers[li + 1]
            if li > 0:
                load_x(li, 0)
            if nt > 1:
                load_x(li, 1)
            A_norm_a(l, 0)
            A_norm_d(l, 0)
            A_tr(l, 0)
            for g in range(4):
                B(l, 0, g)
            for t in range(nt):
                nxt = t + 1 < nt
                prv = t > 0
                lastt = (t == nt - 1) and (nl is not None)
                if nxt:
                    A_norm_a(l, t + 1)
                if lastt:
                    load_win(nl, range(0, 4))
                B(l, t, 4)
                C_scores2(l, t, 0)
                if nxt:
                    A_norm_d(l, t + 1)
                B(l, t, 5)
                B(l, t, 55)
                if prv:
                    E1(l, t - 1)
                B(l, t, 6)
                C_pv(l, t, 0)
                C_pv(l, t, 1)
                C_scores2(l, t, 2)
                B(l, t, 56)
                if prv:
                    E2(l, t - 1, 0)
                    E2(l, t - 1, 1)
                B(l, t, 57)
                if nxt:
                    A_tr(l, t + 1)
                D1(l, t)
                if prv:
                    E3(l, t - 1, last_layer, "a")
                C_pv(l, t, 2)
                C_pv(l, t, 3)
                C_scores2(l, t, 4)
                B(l, t, 7)
                if lastt:
                    load_win(nl, range(4, 8))
                D2(l, t)
                if nxt:
                    B(l, t + 1, 0)
                if prv:
                    E3(l, t - 1, last_layer, "b")
                if t + 2 < nt:
                    load_x(li, t + 2)
                C_pv(l, t, 4)
                C_pv(l, t, 5)
                C_scores2(l, t, 6)
                D3(l, t)
                D4(l, t)
                if nxt:
                    B(l, t + 1, 1)
                    B(l, t + 1, 2)
                C_pv(l, t, 6)
                C_pv(l, t, 7)
                C_fin(l, t, 0)
                C_fin(l, t, 1)
                C_gate(l, t)
                if nxt:
                    B(l, t + 1, 3)
            E1(l, nt - 1)
            E2(l, nt - 1, 0)
            E2(l, nt - 1, 1)
            E3(l, nt - 1, last_layer)

        sem_keys = ["pe", "act", "dve", "pool", "sp"] + sorted(sch.chans)
        sems = {}
        for k in sem_keys:
            sems[k] = es.enter_context(nc.semaphore(k))
        sch.emit(sems)
    return nc, cst


def make_in_maps(inputs, cst, layers_x=None):
    x = np.ascontiguousarray(inputs["x"], dtype=np.float32)
    rbr = np.ascontiguousarray(inputs["rel_bias"][:, :, ::-1], dtype=np.float32).reshape(16, 513)
    common = {
        "pos": np.ascontiguousarray(inputs["positions"], dtype=np.int32),
        "norm_gain": np.ascontiguousarray(inputs["norm_gain"], dtype=np.float32),
        "w_in": np.ascontiguousarray(inputs["w_in"], dtype=np.float32),
        "w_out": np.ascontiguousarray(inputs["w_out"], dtype=np.float32),
        "rbr": rbr,
        "final_gain": np.ascontiguousarray(inputs["final_gain"], dtype=np.float32),
        "c_ident": cst["ident"], "c_invf": cst["invf"], "c_dp": cst["dp"],
        "c_kdec": cst["kdec"], "c_epsg": cst["epsg"], "c_mask": cst["mask"],
    }
    maps = []
    for c in range(NCORES):
        m = dict(common)
        m["x"] = x[c]
        maps.append(m)
    return maps


_CACHE = {}


def kernel(x, positions, norm_gain, w_in, w_out, rel_bias, final_gain):
    inputs = dict(x=np.asarray(x), positions=np.asarray(positions), norm_gain=np.asarray(norm_gain),
                  w_in=np.asarray(w_in), w_out=np.asarray(w_out), rel_bias=np.asarray(rel_bias),
                  final_gain=np.asarray(final_gain))
    if "nc" not in _CACHE:
        _CACHE["nc"] = build_program()
    nc, cst = _CACHE["nc"]
    maps = make_in_maps(inputs, cst)
    res = run_bass_kernel_spmd(nc, maps, core_ids=list(range(NCORES)))
    out = np.stack([np.asarray(r["y"], dtype=np.float32) for r in res.results], axis=0)
    return out.reshape(8, SEQ, D)
```
